# Optimizing a Trainium2 kernel written in Bass

```python
import jax, jax.numpy as jnp
from jax import lax
import numpy as np

D_MODEL = 1024
BATCH = 2
SEQ = 8192
DEPTH = 4

D_MIX = D_MODEL
D_GM = D_MIX // 4
D_RG = D_MIX // 2
D_FT = D_MIX - D_GM - D_RG
GM_HEADS = 4
GM_HEAD_DIM = D_GM // GM_HEADS
GM_CHUNK = 128
RG_HEADS = 8
RG_HEAD_DIM = D_RG // RG_HEADS
RG_CONV = 4
RG_CONV_LEFT = 2
RG_C = 8.0
FT_GROUPS = 4
FT_GROUP_DIM = D_FT // FT_GROUPS
D_FF = 4 * D_MODEL
D_IN = 2 * D_GM + 2 * D_RG + D_FT
SPLITS = (D_GM, 2 * D_GM, 2 * D_GM + D_RG, 2 * D_GM + 2 * D_RG)
EPS = 1e-6

kernel_name = "hybrid_gmlp_rglru_fnet_encoder"


def rms_norm(x, g):
    xf = x.astype(jnp.float32)
    y = xf * lax.rsqrt(jnp.mean(xf * xf, axis=-1, keepdims=True) + EPS)
    return (y * g.astype(jnp.float32)).astype(x.dtype)


def layer_norm(x, g, b):
    xf = x.astype(jnp.float32)
    mu = jnp.mean(xf, axis=-1, keepdims=True)
    var = jnp.mean(jnp.square(xf - mu), axis=-1, keepdims=True)
    y = (xf - mu) * lax.rsqrt(var + EPS)
    return (y * g.astype(jnp.float32) + b.astype(jnp.float32)).astype(x.dtype)


def spatial_gating(u, v, ln_g, ln_b, w_s, b_s):
    bsz, seq, _ = v.shape
    v = layer_norm(v, ln_g, ln_b).reshape(bsz, seq // GM_CHUNK, GM_CHUNK, GM_HEADS, GM_HEAD_DIM)
    s = jnp.einsum("hpq,bnqhd->bnphd", w_s, v) + b_s.T[:, :, None]
    return u * s.reshape(bsz, seq, D_GM)


def centred_depthwise_conv(x, w, b):
    seq = x.shape[1]
    xp = jnp.pad(x, ((0, 0), (RG_CONV_LEFT, RG_CONV - 1 - RG_CONV_LEFT), (0, 0)))
    y = b
    for k in range(RG_CONV):
        y = y + xp[:, k:k + seq] * w[k]
    return y


def _linear_recurrence(c1, c2):
    a1, b1 = c1
    a2, b2 = c2
    return a1 * a2, a2 * b1 + b2


def rg_lru(x, w_a, b_a, w_x, b_x, lam):
    bsz, seq, _ = x.shape
    f32 = jnp.float32
    xh = x.astype(f32).reshape(bsz, seq, RG_HEADS, RG_HEAD_DIM)
    r = jax.nn.sigmoid(jnp.einsum("bshi,hio->bsho", xh, w_a.astype(f32)) + b_a.astype(f32))
    i = jax.nn.sigmoid(jnp.einsum("bshi,hio->bsho", xh, w_x.astype(f32)) + b_x.astype(f32))
    log_a = -RG_C * r * jax.nn.softplus(-lam.astype(f32))
    a = jnp.exp(log_a)
    inp = jnp.sqrt(-jnp.expm1(2.0 * log_a)) * (i * xh)
    _, h = lax.associative_scan(_linear_recurrence, (a, inp), axis=1)
    return h.reshape(bsz, seq, D_RG).astype(x.dtype)


def bidirectional_rg_lru(x, w_a, b_a, w_x, b_x, lam):
    h_fwd = rg_lru(x, w_a[0], b_a[0], w_x[0], b_x[0], lam[0])
    h_bwd = jnp.flip(rg_lru(jnp.flip(x, axis=1), w_a[1], b_a[1], w_x[1], b_x[1], lam[1]), axis=1)
    return h_fwd + h_bwd


def fourier_mix(x, w_f, b_f):
    bsz, seq, _ = x.shape
    f32 = jnp.float32
    xg = x.astype(f32).reshape(bsz, seq, FT_GROUPS, FT_GROUP_DIM)
    y = jnp.fft.fftn(xg, axes=(1, 3), norm="ortho").real
    y = jnp.einsum("bsgi,gio->bsgo", y, w_f.astype(f32)) + b_f.astype(f32)
    return y.reshape(bsz, seq, D_FT).astype(x.dtype)


def setup_inputs(seed: int = 0) -> dict:
    key = jax.random.key(seed)
    ks = jax.random.split(key, 26)
    f32 = jnp.float32

    def nrm(k, shape, scale):
        return jax.random.normal(k, shape, f32) * scale

    def gain(k, shape):
        return 1.0 + 0.05 * jax.random.normal(k, shape, f32)

    u = jax.random.uniform(ks[17], (DEPTH, 2, RG_HEADS, RG_HEAD_DIM), f32, minval=0.9, maxval=0.999)
    a_base = u ** (1.0 / RG_C)
    rg_lam = jnp.log(a_base) - jnp.log1p(-a_base)

    return {
        "x": nrm(ks[0], (BATCH, SEQ, D_MODEL), 1.0),
        "c": nrm(ks[1], (BATCH, D_MODEL), 1.0),
        "w_ada": nrm(ks[2], (DEPTH, D_MODEL, 6 * D_MODEL), 0.5 * D_MODEL ** -0.5),
        "b_ada": nrm(ks[3], (DEPTH, 6 * D_MODEL), 0.01),
        "g_pre_mix": gain(ks[4], (DEPTH, D_MODEL)),
        "g_post_mix": gain(ks[5], (DEPTH, D_MODEL)),
        "w_in": nrm(ks[6], (DEPTH, D_MODEL, D_IN), D_MODEL ** -0.5),
        "gm_ln_g": gain(ks[7], (DEPTH, D_GM)),
        "gm_ln_b": nrm(ks[8], (DEPTH, D_GM), 0.01),
        "gm_w_s": nrm(ks[9], (DEPTH, GM_HEADS, GM_CHUNK, GM_CHUNK), GM_CHUNK ** -0.5),
        "gm_b_s": gain(ks[10], (DEPTH, GM_HEADS, GM_CHUNK)),
        "rg_conv_w": nrm(ks[11], (DEPTH, RG_CONV, D_RG), RG_CONV ** -0.5),
        "rg_conv_b": nrm(ks[12], (DEPTH, D_RG), 0.01),
        "rg_w_a": nrm(ks[13], (DEPTH, 2, RG_HEADS, RG_HEAD_DIM, RG_HEAD_DIM), RG_HEAD_DIM ** -0.5),
        "rg_b_a": nrm(ks[14], (DEPTH, 2, RG_HEADS, RG_HEAD_DIM), 0.01),
        "rg_w_x": nrm(ks[15], (DEPTH, 2, RG_HEADS, RG_HEAD_DIM, RG_HEAD_DIM), RG_HEAD_DIM ** -0.5),
        "rg_b_x": nrm(ks[16], (DEPTH, 2, RG_HEADS, RG_HEAD_DIM), 0.01),
        "rg_lam": rg_lam,
        "ft_w": nrm(ks[18], (DEPTH, FT_GROUPS, FT_GROUP_DIM, FT_GROUP_DIM), FT_GROUP_DIM ** -0.5),
        "ft_b": nrm(ks[19], (DEPTH, FT_GROUPS, FT_GROUP_DIM), 0.01),
        "g_mix_out": gain(ks[20], (DEPTH, D_MIX)),
        "w_out": nrm(ks[21], (DEPTH, D_MIX, D_MODEL), D_MIX ** -0.5),
        "g_pre_ff": gain(ks[22], (DEPTH, D_MODEL)),
        "g_post_ff": gain(ks[23], (DEPTH, D_MODEL)),
        "w_ff1": nrm(ks[24], (DEPTH, D_MODEL, D_FF), D_MODEL ** -0.5),
        "w_ff2": nrm(ks[25], (DEPTH, D_FF, D_MODEL), D_FF ** -0.5),
    }


def reference(x, c, w_ada, b_ada, g_pre_mix, g_post_mix, w_in, gm_ln_g, gm_ln_b, gm_w_s, gm_b_s,
              rg_conv_w, rg_conv_b, rg_w_a, rg_b_a, rg_w_x, rg_b_x, rg_lam, ft_w, ft_b,
              g_mix_out, w_out, g_pre_ff, g_post_ff, w_ff1, w_ff2):
    cond = jax.nn.silu(c)
    for l in range(DEPTH):
        mod = (cond @ w_ada[l] + b_ada[l])[:, None, :]
        sh1, sc1, gt1, sh2, sc2, gt2 = jnp.split(mod, 6, axis=-1)

        h = rms_norm(x, g_pre_mix[l]) * (1.0 + sc1) + sh1
        z = h @ w_in[l]
        u, v, rg_g, rg_x, ft_x = jnp.split(z, SPLITS, axis=-1)
        y_gm = spatial_gating(jax.nn.gelu(u), jax.nn.gelu(v), gm_ln_g[l], gm_ln_b[l], gm_w_s[l], gm_b_s[l])
        xr = centred_depthwise_conv(rg_x, rg_conv_w[l], rg_conv_b[l])
        y_rg = bidirectional_rg_lru(xr, rg_w_a[l], rg_b_a[l], rg_w_x[l], rg_b_x[l], rg_lam[l]) * jax.nn.gelu(rg_g)
        y_ft = fourier_mix(ft_x, ft_w[l], ft_b[l])
        gm = g_mix_out[l]
        y = jnp.concatenate([
            rms_norm(y_gm, gm[:D_GM]),
            rms_norm(y_rg, gm[D_GM:D_GM + D_RG]),
            rms_norm(y_ft, gm[D_GM + D_RG:]),
        ], axis=-1)
        x = x + gt1 * rms_norm(y @ w_out[l], g_post_mix[l])

        h = rms_norm(x, g_pre_ff[l]) * (1.0 + sc2) + sh2
        f = jnp.square(jax.nn.relu(h @ w_ff1[l])) @ w_ff2[l]
        x = x + gt2 * rms_norm(f, g_post_ff[l])
    return x
```

```python
import math
from contextlib import ExitStack

import numpy as np
import concourse.bass as bass
import concourse.mybir as mybir
from concourse.bass_utils import run_bass_kernel_spmd

F32 = mybir.dt.float32
BF16 = mybir.dt.bfloat16
AF = mybir.ActivationFunctionType
ALU = mybir.AluOpType

DEPTH = 4
D = 1024
T = 2048
SEQ = 8192
NBLK = 4
D_IN = 1792
D_FF = 4096
EPS = 1e-6
NP = 152
SAME_ENG_SYNC = True
NDS = 12
STOP = 99

PP_BADA = 0
PP_GPRE1 = 48
PP_GPOST1 = 56
PP_GPRE2 = 64
PP_GPOST2 = 72
PP_GMO = 80
PP_CW = 88
PP_CB = 104
PP_BA = 108
PP_BX = 116
PP_LAM = 124
PP_FTB = 132


class StopBuild(Exception):
    pass


class V:
    def __init__(self, ap, space, gran, off, shape, strides, esz, lead=1):
        self.ap = ap
        self.lead = lead
        self.space = space
        self.gran = gran
        self.off = off
        self.shape = list(shape)
        self.strides = list(strides)
        self.esz = esz

    def __getitem__(self, idx):
        if not isinstance(idx, tuple):
            idx = (idx,)
        off = self.off
        shape, strides, ap_idx = [], [], [slice(None)] * self.lead
        for d in range(len(self.shape)):
            i = idx[d] if d < len(idx) else slice(None)
            if isinstance(i, int):
                off += i * self.strides[d]
                ap_idx.append(i)
            else:
                s, e, st = i.indices(self.shape[d])
                assert st == 1
                off += s * self.strides[d]
                shape.append(e - s)
                strides.append(self.strides[d])
                ap_idx.append(slice(s, e))
        return V(self.ap[tuple(ap_idx)], self.space, self.gran, off, shape, strides, self.esz, self.lead)

    def keys(self):
        out = set()

        def rec(off, shape, strides):
            ext = sum((n - 1) * s for n, s in zip(shape, strides)) + self.esz
            if len(shape) >= 2 and strides[0] >= 2 * self.gran and shape[0] <= 64 and \
                    strides[0] > ext - (shape[0] - 1) * strides[0]:
                for i in range(shape[0]):
                    rec(off + i * strides[0], shape[1:], strides[1:])
            else:
                out.update(range(off // self.gran, (off + ext - 1) // self.gran + 1))
        rec(self.off, self.shape, self.strides)
        return [(self.space, g) for g in out]

    def w(self, ap):
        return V(ap, self.space, self.gran, self.off, self.shape, self.strides, self.esz, self.lead)

    def parts(self, lo, hi):
        return self.w(self.ap[lo:hi])


def mkview(ap, space, gran, shape, esz, off=0, lead=1):
    strides = []
    s = esz
    for n in reversed(shape):
        strides.append(s)
        s *= n
    strides.reverse()
    return V(ap, space, gran, off, shape, strides, esz, lead)


class Sched:
    ENG = ("pe", "act", "dve", "pool", "sp")

    def __init__(self):
        self.ops = {e: [] for e in self.ENG}
        self.cnt = {}
        self.writers = {}
        self.readers = {}
        self.seen = {e: {} for e in self.ENG}
        self.epoch = 0
        self.dma_idx = {}

    def op(self, eng, fn, reads=(), writes=(), dma=None, inc=None):
        rk, wk = set(), set()
        for v in reads:
            rk.update(v.keys())
        for v in writes:
            wk.update(v.keys())
        deps = {}

        def need(d):
            for s, val in d.items():
                if val > deps.get(s, 0):
                    deps[s] = val

        for k in rk:
            if k in self.writers:
                need(self.writers[k])
        for k in wk:
            if k in self.writers:
                need(self.writers[k])
            if k in self.readers:
                need(self.readers[k])
        if dma == "cc":
            sem, inc = "cc", 1
            deps[sem] = max(deps.get(sem, 0), self.cnt.get(sem, 0))
        elif dma:
            i = self.dma_idx.get(eng, 0)
            self.dma_idx[eng] = i + 1
            sem = f"d{eng}{i % NDS}"
            inc = 16
            deps[sem] = max(deps.get(sem, 0), self.cnt.get(sem, 0))
        else:
            sem = f"c{eng}_{self.epoch}"
            inc = 1
        waits = []
        for s, val in deps.items():
            if val <= 0:
                continue
            if not dma and s.startswith("c" + eng + "_"):
                if eng == "pe" or not SAME_ENG_SYNC:
                    continue
            if val > self.seen[eng].get(s, 0):
                self.seen[eng][s] = val
                waits.append((s, val))
        self.cnt[sem] = self.cnt.get(sem, 0) + inc
        me = self.cnt[sem]
        for k in wk:
            self.writers.setdefault(k, {})[sem] = me
            self.readers[k] = {}
        for k in rk:
            if k not in wk:
                self.readers.setdefault(k, {})[sem] = me
        self.ops[eng].append((waits, fn, sem, inc))
        return (sem, me)

    def emit(self, eng_name, e, sems):
        for waits, fn, sem, inc in self.ops[eng_name]:
            for s, val in waits:
                e.wait_ge(sems[s], val)
            ins = fn(e)
            ins.then_inc(sems[sem], inc)


def build_program(depth_run=DEPTH):
    nc = bass.Bass("TRN2", target_bir_lowering=False)
    S = Sched()
    es = ExitStack()

    WL = {k: (depth_run if STOP >= v else 1) for k, v in
          (("w_ada", 0), ("w_in", 1), ("w_out", 7), ("w_ff1", 8), ("w_ff2", 8))}

    def dram_in(name, shape, dt=F32):
        return nc.dram_tensor(name, list(shape), dt, kind="ExternalInput").ap()

    d_xT = dram_in("xT", [D, T])
    d_cT = dram_in("cT", [128, 8])
    d_wada = dram_in("w_ada", [WL['w_ada'], D, 6 * D])
    d_win = dram_in("w_in", [WL['w_in'], D, D_IN])
    d_wout = dram_in("w_out", [WL['w_out'], D, D])
    d_wff1 = dram_in("w_ff1", [WL['w_ff1'], D, D_FF])
    d_wff2 = dram_in("w_ff2", [WL['w_ff2'], D_FF, D])
    d_pp = dram_in("pp", [DEPTH, 128, NP])
    d_gmln = dram_in("gmln", [DEPTH, 128, 512])
    d_wsT = dram_in("wsT", [DEPTH, 128, 512])
    d_bsrow = dram_in("bsrow", [DEPTH, 1, 512])
    d_rgw = dram_in("rgw", [DEPTH, 128, 2048])
    d_ftwbd = dram_in("ftwbd", [DEPTH, 128, 256])
    d_cs1 = dram_in("cs1", [128, 128])
    d_fb = dram_in("fb", [128, 8192])
    d_c64 = dram_in("c64bd", [128, 256])
    d_masks = dram_in("masks", [128, 12])
    d_out = nc.dram_tensor("outT", [D, T], F32, kind="ExternalOutput").ap()

    def dram_scr(name, shape, dt, gran):
        ap = nc.dram_tensor(name, list(shape), dt).ap()
        esz = 2 if dt == BF16 else 4
        return mkview(ap, name, gran, list(shape), esz, lead=0)

    ftx_send = dram_scr("ftx_send", [2 * T, 128], BF16, 1 << 30)
    ftx_all = dram_scr("ftx_all", [8 * T, 128], BF16, 1 << 30)
    halo_send = dram_scr("halo_send", [512, 3], F32, 1 << 30)
    halo_all = dram_scr("halo_all", [2048, 3], F32, 1 << 30)
    carr_send = dram_scr("carr_send", [128, 16], F32, 1 << 30)
    carr_all = dram_scr("carr_all", [512, 16], F32, 1 << 30)
    rgxD = dram_scr("rgxD", [512, T], F32, 128 * T * 4)
    ggD = dram_scr("ggD", [512, T], F32, 1 << 30)
    aD = dram_scr("aD", [2, 512, T], F32, 128 * T * 4)
    inpD = dram_scr("inpD", [2, 512, T], F32, 128 * T * 4)

    def sb(name, shape, dt, gran=None):
        t = es.enter_context(nc.sbuf_tensor("s_" + name, [128] + list(shape), dt))
        esz = 2 if dt == BF16 else 4
        nbytes = esz * int(np.prod(shape))
        return mkview(t[:], name, gran or nbytes, list(shape), esz)

    xT = sb("xT", [8, T], F32, gran=2048)
    cond = sb("cond", [8], F32)
    ones_bf = sb("ones_bf", [128], BF16)
    ones_f = sb("ones_f", [64], F32)
    cs1_bf = sb("cs1_bf", [4, 32], BF16)
    c64 = sb("c64", [256], F32)
    masks = sb("masks", [12], F32)
    pp = sb("pp", [NP], F32)
    gmln = sb("gmln", [512], F32)
    bsrow = sb("bsrow", [512], F32)
    modT = sb("modT", [48], F32)
    der = sb("der", [64], F32)
    wsT_bf = sb("wsT_bf", [512], BF16)
    rgw_bf = sb("rgw_bf", [16, 128], BF16)
    ftwbd = sb("ftwbd", [256], F32)
    wcs_bf = sb("wcs_bf", [2, 2, 128], BF16)
    sumr = sb("sumr", [2, 4], F32, gran=4)
    carr_sb = sb("carr_sb", [2, 4, 2], F32)
    carr_g = sb("carr_g", [4, 16], F32)
    halo_sb = sb("halo_sb", [4, 4, 3], F32)
    cst = sb("cst", [16], F32)
    hin = sb("hin", [2, 4], F32)
    small = sb("small", [16], F32, gran=4)

    DG1, DG2, DGT1, DGT2, DGMO, DCSP, DTMP = 0, 8, 16, 24, 32, 40, 48

    ARENA = 126 * 1024
    AR_t = es.enter_context(nc.sbuf_tensor("arena", [128, ARENA // 2], BF16))

    def ar(off, shape, dt):
        esz = 2 if dt == BF16 else 4
        n = int(np.prod(shape))
        assert off % 4 == 0 and off + n * esz <= ARENA, (off, shape)
        ap = AR_t[:, off // 2:(off + n * esz) // 2]
        if dt != BF16:
            ap = ap.bitcast(dt)
        if len(shape) == 2:
            ap = ap.rearrange("p (a b) -> p a b", a=shape[0])
        elif len(shape) == 3:
            ap = ap.rearrange("p (a b c) -> p a b c", a=shape[0], b=shape[1])
        elif len(shape) == 4:
            ap = ap.rearrange("p (a b c d) -> p a b c d", a=shape[0], b=shape[1], c=shape[2])
        return mkview(ap, "arena", 1024, list(shape), esz, off=off)

    K = 1024
    yT_bf = ar(0, [8, T], BF16)
    win_bf = ar(32 * K, [8, D_IN], BF16)
    hT = [ar(60 * K, [8, 512], BF16), ar(68 * K, [8, 512], BF16)]
    guT = ar(76 * K, [2, T], F32)
    vn_sb = ar(92 * K, [16, 256], BF16)
    stg = [ar(100 * K, [4, 512], F32), ar(108 * K, [4, 512], F32)]
    sq_bf = ar(116 * K, [8, 512], BF16)
    Rb = ar(124 * K, [512], F32)

    ps_t = es.enter_context(nc.psum_tensor("ps", [128, 8, 512], F32))
    PS = mkview(ps_t[:], "ps", 2048, [8, 512], 4)
    ps_free = list(range(8))

    class Bank:
        def __init__(self):
            self.i = ps_free.pop(0)
            self.v = PS[self.i]

        def free(self):
            ps_free.append(self.i)

    def R(v):
        return v.ap

    def act(out, in_, func, scale=1.0, bias=None, accum=None, eng="act"):
        reads = [in_]
        kw = {}
        if isinstance(scale, V):
            reads.append(scale)
            kw["scale"] = scale.ap
        else:
            kw["scale"] = float(scale)
        if isinstance(bias, V):
            reads.append(bias)
            kw["bias"] = bias.ap
        elif bias is not None:
            kw["bias"] = float(bias)
        writes = [out]
        if accum is not None:
            writes.append(accum)
            kw["accum_out"] = accum.ap
        S.op("act", lambda e: e.activation(out=out.ap, in_=in_.ap, func=func, **kw), reads, writes)

    def tt(out, a, b, op, eng="dve"):
        S.op(eng, lambda e: e.tensor_tensor(out=out.ap, in0=a.ap, in1=b.ap, op=op), [a, b], [out])

    def ts(out, a, s1, s2, op0, op1=None, eng="dve"):
        reads = [a]
        s1v = s1.ap if isinstance(s1, V) else float(s1)
        if isinstance(s1, V):
            reads.append(s1)
        s2v = None
        if s2 is not None:
            s2v = s2.ap if isinstance(s2, V) else float(s2)
            if isinstance(s2, V):
                reads.append(s2)
        if op1 is None:
            S.op(eng, lambda e: e.tensor_scalar(out=out.ap, in0=a.ap, scalar1=s1v, scalar2=None, op0=op0),
                 reads, [out])
        else:
            S.op(eng, lambda e: e.tensor_scalar(out=out.ap, in0=a.ap, scalar1=s1v, scalar2=s2v, op0=op0, op1=op1),
                 reads, [out])

    def stt(out, in0, scalar, in1, op0, op1, eng="dve"):
        reads = [in0, in1]
        sv = scalar.ap if isinstance(scalar, V) else float(scalar)
        if isinstance(scalar, V):
            reads.append(scalar)
        S.op(eng, lambda e: e.scalar_tensor_tensor(out=out.ap, in0=in0.ap, scalar=sv, in1=in1.ap, op0=op0, op1=op1),
             reads, [out])

    def cp(out, in_, eng="dve"):
        S.op(eng, lambda e: e.tensor_copy(out=out.ap, in_=in_.ap), [in_], [out])

    def rsum(out, in_):
        S.op("dve", lambda e: e.tensor_reduce(out=out.ap, in_=in_.ap, axis=mybir.AxisListType.X, op=ALU.add),
             [in_], [out])

    def recip(out, in_):
        S.op("dve", lambda e: e.reciprocal(out=out.ap, in_=in_.ap), [in_], [out])

    def memset(out, val, eng="dve"):
        S.op(eng, lambda e: e.memset(out.ap, val), [], [out])

    def scan(out, a, b, init):
        reads = [a, b]
        iv = init.ap if isinstance(init, V) else float(init)
        if isinstance(init, V):
            reads.append(init)
        S.op("dve", lambda e: e.tensor_tensor_scan(out=out.ap, data0=a.ap, data1=b.ap, initial=iv,
                                                   op0=ALU.mult, op1=ALU.add), reads, [out])

    def mm(group):
        reads, writes = [], []
        for o, l, r, st, sp in group:
            reads += [l, r]
            writes.append(o)

        def fn(e):
            ins = None
            for o, l, r, st, sp in group:
                ins = e.matmul(o.ap, lhsT=l.ap, rhs=r.ap, start=st, stop=sp)
            return ins
        S.op("pe", fn, reads, writes)

    def dma(q, out, in_, sem, reads=None, writes=None, **kw):
        reads = reads if reads is not None else ([in_] if isinstance(in_, V) else [])
        writes = writes if writes is not None else ([out] if isinstance(out, V) else [])
        oap = out.ap if isinstance(out, V) else out
        iap = in_.ap if isinstance(in_, V) else in_
        S.op(q, lambda e: e.dma_start(out=oap, in_=iap, **kw), reads, writes, dma=sem)

    def allgather(in_v, out_v):
        S.op("pool", lambda e: e.collective_compute("AllGather", ALU.bypass,
                                                    replica_groups=[[0, 1, 2, 3], [4, 5, 6, 7]],
                                                    ins=[in_v.ap], outs=[out_v.ap]),
             [in_v], [out_v], dma="cc", inc=1)

    def sumsq_rstd(src3, nch, bank_cols=512, dtot=None):
        act(sq_bf[0:nch], src3, AF.Square)
        bk = Bank()
        mm([(bk.v, ones_bf, sq_bf[c], c == 0, c == nch - 1) for c in range(nch)])
        act(Rb, bk.v, AF.Sqrt, bias=float(dtot * EPS))
        recip(Rb, Rb)
        bk.free()

    for c in range(8):
        dma("sp", xT[c], d_xT[c * 128:(c + 1) * 128, :], "dx")
    dma("sp", cond, d_cT, "dsm")
    dma("sp", c64, d_c64, "dsm")
    dma("sp", masks, d_masks, "dsm")
    dma("pool", cs1_bf.w(cs1_bf.ap.rearrange("p a b -> p (a b)")), d_cs1, "dw")
    memset(ones_bf, 1.0)
    memset(ones_f, 1.0)
    act(cond, cond, AF.Silu)

    out_sems = []

    def chk(level):
        if STOP < level:
            raise StopBuild()

    def layer(l):
        S.epoch = l
        dma("sp", pp, d_pp[l], "dsm")
        dma("sp", gmln, d_gmln[l], "dsm")
        dma("sp", bsrow.parts(0, 1), d_bsrow[l], "dsm")
        dma("sp", ftwbd, d_ftwbd[l], "dsm")
        dma("pool", wsT_bf, d_wsT[l], "dw")
        dma("pool", rgw_bf.w(rgw_bf.ap.rearrange("p a b -> p (a b)")), d_rgw[l], "dw",
            max_dma_last_dim=8192)
        wada_bufs = [ar(60 * K, [8, 512], F32), ar(76 * K, [8, 512], F32)]
        mbank = Bank()
        for g in range(12):
            wb = wada_bufs[g % 2]
            dma("sp", wb, d_wada[l][:, g * 512:(g + 1) * 512].rearrange("(kc p) m -> p kc m", p=128), "dwa")
            for mc in range(4):
                col = g * 4 + mc
                mm([(mbank.v[col:col + 1], wb[kc, mc * 128:(mc + 1) * 128], cond[kc:kc + 1], kc == 0, kc == 7)
                    for kc in range(8)])
        tt(modT, mbank.v[0:48], pp[PP_BADA:PP_BADA + 48], ALU.add)
        mbank.free()
        SH1, SC1, GT1, SH2, SC2, GT2 = (modT[i * 8:(i + 1) * 8] for i in range(6))
        sqD = math.sqrt(D)
        ts(der[DG1:DG1 + 8], SC1, 1.0, sqD, ALU.add, ALU.mult)
        tt(der[DG1:DG1 + 8], der[DG1:DG1 + 8], pp[PP_GPRE1:PP_GPRE1 + 8], ALU.mult)
        ts(der[DG2:DG2 + 8], SC2, 1.0, sqD, ALU.add, ALU.mult)
        tt(der[DG2:DG2 + 8], der[DG2:DG2 + 8], pp[PP_GPRE2:PP_GPRE2 + 8], ALU.mult)
        stt(der[DGT1:DGT1 + 8], GT1, sqD, pp[PP_GPOST1:PP_GPOST1 + 8], ALU.mult, ALU.mult)
        stt(der[DGT2:DGT2 + 8], GT2, sqD, pp[PP_GPOST2:PP_GPOST2 + 8], ALU.mult, ALU.mult)
        ts(der[DGMO:DGMO + 2], pp[PP_GMO:PP_GMO + 2], 16.0, None, ALU.mult)
        ts(der[DGMO + 2:DGMO + 6], pp[PP_GMO + 2:PP_GMO + 6], math.sqrt(512.0), None, ALU.mult)
        ts(der[DGMO + 6:DGMO + 8], pp[PP_GMO + 6:PP_GMO + 8], 16.0, None, ALU.mult)
        act(der[DTMP:DTMP + 8], pp[PP_LAM:PP_LAM + 8], AF.Exp, scale=-1.0)
        act(der[DTMP:DTMP + 8], der[DTMP:DTMP + 8], AF.Ln, bias=1.0)
        ts(der[DCSP:DCSP + 8], der[DTMP:DTMP + 8], -8.0, None, ALU.mult)

        chk(1)
        for kc in range(8):
            dma("pool", win_bf[kc], d_win[l][kc * 128:(kc + 1) * 128, :], "dw")
        for blk in range(NBLK):
            bs = slice(blk * 512, (blk + 1) * 512)
            h = hT[blk % 2]
            chk(1.1)
            sumsq_rstd(xT[:, bs], 8, dtot=D)
            chk(1.2)
            tmp8 = [stg[0][i] for i in range(4)] + [stg[1][i] for i in range(4)]
            for kc in range(8):
                tt(tmp8[kc], xT[kc, bs], Rb, ALU.mult)
                act(h[kc], tmp8[kc], AF.Identity, scale=der[DG1 + kc:DG1 + kc + 1], bias=SH1[kc:kc + 1])
            chk(1.3)
            for mo in (0, 1):
                bk = Bank()
                mm([(bk.v, win_bf[kc, mo * 128:(mo + 1) * 128], h[kc], kc == 0, kc == 7) for kc in range(8)])
                act(guT[mo, bs], bk.v, AF.Gelu_apprx_tanh)
                bk.free()
            for t in range(4):
                mo = 4 + t
                bk = Bank()
                mm([(bk.v, win_bf[kc, mo * 128:(mo + 1) * 128], h[kc], kc == 0, kc == 7) for kc in range(8)])
                act(stg[0][t], bk.v, AF.Gelu_apprx_tanh)
                bk.free()
            dma("sp", ggD.w(ggD.ap[:, bs].rearrange("(t p) n -> p t n", p=128)), stg[0], "dst")
            for t in range(4):
                mo = 8 + t
                bk = Bank()
                mm([(bk.v, win_bf[kc, mo * 128:(mo + 1) * 128], h[kc], kc == 0, kc == 7) for kc in range(8)])
                cp(stg[1][t], bk.v)
                bk.free()
            dma("sp", rgxD.w(rgxD.ap[:, bs].rearrange("(t p) n -> p t n", p=128)), stg[1], "dst",
                writes=[rgxD[t * 128:(t + 1) * 128] for t in range(4)])
            chk(1.5)
            for pc in range(4):
                ch = blk * 4 + pc
                tsl = slice(pc * 128, (pc + 1) * 128)
                bk = Bank()
                mm([(bk.v[0:256], h[kc, tsl], win_bf[kc, 256:512], kc == 0, kc == 7) for kc in range(8)] +
                   [(bk.v[256:512], h[kc, tsl], win_bf[kc, 1536:1792], kc == 0, kc == 7) for kc in range(8)])
                chk(1.55)
                scr = ar(28 * K, [2, 256], F32)
                ftst = ar(30 * K + (pc % 2) * 512, [256], BF16)
                gv, junk = scr[0], scr[1]
                act(gv, bk.v[0:256], AF.Gelu_apprx_tanh)
                rsum(small[0:1], gv)
                chk(1.57)
                cp(ftst, bk.v[256:512])
                bk.free()
                chk(1.6)
                for hh_ in range(2):
                    dma("sp", ftx_send.w(ftx_send.ap[hh_ * T + ch * 128:hh_ * T + (ch + 1) * 128, :]),
                        ftst[hh_ * 128:(hh_ + 1) * 128], "dst", writes=[ftx_send])
                chk(1.7)
                ts(small[1:2], small[0:1], -1.0 / 256.0, None, ALU.mult)
                ts(gv, gv, small[1:2], None, ALU.add)
                tt(junk, gv, gv, ALU.mult)
                rsum(small[2:3], junk)
                ts(small[3:4], small[2:3], 1.0 / 256.0, EPS, ALU.mult, ALU.add)
                act(small[4:5], small[3:4], AF.Sqrt)
                recip(small[4:5], small[4:5])
                stt(gv, gv, small[4:5], gmln[0:256], ALU.mult, ALU.mult)
                tt(vn_sb[ch], gv, gmln[256:512], ALU.add)
                chk(1.8)

        chk(2)
        dma("sp", halo_send.w(halo_send.ap[:, 0:1]), rgxD.w(rgxD.ap[:, 0:1]), "dst",
            reads=[rgxD[t * 128:(t + 1) * 128] for t in range(4)], allow_slow_non_contiguous=True)
        dma("sp", halo_send.w(halo_send.ap[:, 1:3]), rgxD.w(rgxD.ap[:, T - 2:T]), "dst",
            reads=[rgxD[t * 128:(t + 1) * 128] for t in range(4)], allow_slow_non_contiguous=True)
        allgather(halo_send, halo_all)
        allgather(ftx_send, ftx_all)

        chk(3)
        ygm = stg[0]
        for blk in range(NBLK):
            bs = slice(blk * 512, (blk + 1) * 512)
            banks = [Bank(), Bank()]
            for pc in range(4):
                ch = blk * 4 + pc
                for tile in range(2):
                    for hh in range(2):
                        hd = tile * 2 + hh
                        o = banks[tile].v[pc * 128:(pc + 1) * 128].parts(64 * hh, 64 * hh + 64)
                        mm([(o, vn_sb[ch, hd * 64:(hd + 1) * 64], wsT_bf[hd * 128:(hd + 1) * 128], True, False),
                            (o, ones_f.parts(0, 1), bsrow[hd * 128:(hd + 1) * 128].parts(0, 1), False, True)])
            for tile in range(2):
                tt(ygm[tile], guT[tile, bs], banks[tile].v, ALU.mult)
                banks[tile].free()
            sumsq_rstd(ygm[0:2], 2, dtot=256)
            for tile in range(2):
                stt(yT_bf[tile, bs], ygm[tile], der[DGMO + tile:DGMO + tile + 1], Rb, ALU.mult, ALU.mult)

        chk(4)
        rgx_t = ar(32 * K, [T + 4], F32)
        xr = ar(41 * K, [1024], F32)
        xr_bf = ar(45 * K, [1024], BF16)
        r_sb = ar(47 * K, [1024], F32)
        i_sb = ar(51 * K, [1024], F32)
        t1 = ar(55 * K, [1024], F32)
        t2 = ar(59 * K, [1024], F32)
        a_full = ar(63 * K, [2, T], F32)
        inp_full = ar(79 * K, [2, T], F32)
        hscr = ar(95 * K, [T], F32)
        dma("sp", halo_sb.w(halo_sb.ap.rearrange("p r t c -> p (r t) c")),
            halo_all.w(halo_all.ap.rearrange("(rt p) c -> p rt c", p=128)), "dsm")
        for t in range(4):
            dma("sp", rgx_t[2:T + 2], rgxD.w(rgxD.ap[t * 128:(t + 1) * 128, :]), "dsm",
                reads=[rgxD[t * 128:(t + 1) * 128]])
            ts(rgx_t[0:2], halo_sb[0, t, 1:3], masks[4:5], None, ALU.mult)
            ts(rgx_t[T + 2:T + 3], halo_sb[0, t, 0:1], masks[8:9], None, ALU.mult)
            for r in range(1, 4):
                stt(rgx_t[0:2], halo_sb[r, t, 1:3], masks[4 + r:5 + r], rgx_t[0:2], ALU.mult, ALU.add)
                stt(rgx_t[T + 2:T + 3], halo_sb[r, t, 0:1], masks[8 + r:9 + r], rgx_t[T + 2:T + 3],
                    ALU.mult, ALU.add)
            for hf in range(2):
                b0 = hf * 1024
                act(xr, rgx_t[b0:b0 + 1024], AF.Identity, scale=pp[PP_CW + t:PP_CW + t + 1],
                    bias=pp[PP_CB + t:PP_CB + t + 1])
                for k in range(1, 4):
                    stt(xr, rgx_t[b0 + k:b0 + k + 1024], pp[PP_CW + 4 * k + t:PP_CW + 4 * k + t + 1], xr,
                        ALU.mult, ALU.add)
                act(xr_bf, xr, AF.Copy)
                for d in range(2):
                    dt_ = d * 4 + t
                    bks = [Bank(), Bank(), Bank(), Bank()]
                    for c2 in range(2):
                        cs = slice(c2 * 512, (c2 + 1) * 512)
                        mm([(bks[c2].v, rgw_bf[(d * 2 + 0) * 4 + t], xr_bf[cs], True, True)])
                        mm([(bks[2 + c2].v, rgw_bf[(d * 2 + 1) * 4 + t], xr_bf[cs], True, True)])
                    for c2 in range(2):
                        cs = slice(c2 * 512, (c2 + 1) * 512)
                        act(r_sb[cs], bks[c2].v, AF.Sigmoid, bias=pp[PP_BA + dt_:PP_BA + dt_ + 1])
                        rsum(sumr[d, hf * 2 + c2:hf * 2 + c2 + 1], r_sb[cs])
                        bks[c2].free()
                    for c2 in range(2):
                        cs = slice(c2 * 512, (c2 + 1) * 512)
                        act(i_sb[cs], bks[2 + c2].v, AF.Sigmoid, bias=pp[PP_BX + dt_:PP_BX + dt_ + 1])
                        bks[2 + c2].free()
                    af = a_full[d, b0:b0 + 1024]
                    act(af, r_sb, AF.Exp, scale=der[DCSP + dt_:DCSP + dt_ + 1])
                    tt(t1, af, af, ALU.mult)
                    act(t1, t1, AF.Sqrt, scale=-1.0, bias=1.0)
                    tt(t2, i_sb, xr, ALU.mult)
                    tt(inp_full[d, b0:b0 + 1024], t2, t1, ALU.mult)
            scan(hscr, a_full[0], inp_full[0], 0.0)
            cp(carr_sb[0, t, 0:1], hscr[T - 1:T])
            scan(hscr.w(hscr.ap[:, ::-1]), a_full[1].w(a_full[1].ap[:, ::-1]),
                 inp_full[1].w(inp_full[1].ap[:, ::-1]), 0.0)
            cp(carr_sb[1, t, 0:1], hscr[0:1])
            for d in range(2):
                dt_ = d * 4 + t
                S.op("dve", lambda e, d=d: e.tensor_reduce(out=cst[d:d + 1].ap, in_=sumr[d].ap,
                                                           axis=mybir.AxisListType.X, op=ALU.add),
                     [sumr[d]], [cst[d:d + 1]])
                act(carr_sb[d, t, 1:2], cst[d:d + 1], AF.Exp, scale=der[DCSP + dt_:DCSP + dt_ + 1])
                dma("sp", aD.w(aD.ap[d, t * 128:(t + 1) * 128, :]), a_full[d], "dst",
                    writes=[aD[d, t * 128:(t + 1) * 128]])
                dma("sp", inpD.w(inpD.ap[d, t * 128:(t + 1) * 128, :]), inp_full[d], "dst",
                    writes=[inpD[d, t * 128:(t + 1) * 128]])
        dma("sp", carr_send, carr_sb.w(carr_sb.ap.rearrange("p d t c -> p (d t c)")), "dst")
        allgather(carr_send, carr_all)
        dma("sp", carr_g, carr_all.w(carr_all.ap.rearrange("(r p) c -> p r c", p=128)), "dsm")
        cg = carr_g.w(carr_g.ap.rearrange("p r (d t c) -> p r d t c", d=2, t=4))

        def cgv(r, d, c):
            return carr_g.w(cg.ap[:, r, d, :, c])

        cc = cst[4:8]
        memset(cc, 0.0)
        memset(hin[0], 0.0)
        for r in range(4):
            stt(hin[0], cc, masks[r:r + 1], hin[0], ALU.mult, ALU.add)
            if r < 3:
                tt(cc, cc, cgv(r, 0, 1), ALU.mult)
                tt(cc, cc, cgv(r, 0, 0), ALU.add)
        memset(cc, 0.0)
        memset(hin[1], 0.0)
        for r in (3, 2, 1, 0):
            stt(hin[1], cc, masks[r:r + 1], hin[1], ALU.mult, ALU.add)
            if r > 0:
                tt(cc, cc, cgv(r, 1, 1), ALU.mult)
                tt(cc, cc, cgv(r, 1, 0), ALU.add)

        chk(5)
        X_sb = ar(32 * K, [128, 128], BF16)
        T_blk = ar(64 * K, [256, 32], BF16)
        fb_bf = ar(80 * K, [2, 64, 64], BF16)
        PQ = ar(96 * K, [2, 2, T], BF16)
        yft = ar(112 * K, [2, 512], F32)
        for h_ in range(2):
            for r in range(4):
                p0 = h_ * 64 + r * 16
                src = ftx_all.ap[(r * 2 + h_) * T:(r * 2 + h_ + 1) * T, :].rearrange("(a b) c -> a (b c)", a=16)
                dst = X_sb.ap[p0:p0 + 16].rearrange("p a b -> p (a b)")
                dma("sp", X_sb.w(dst), ftx_all.w(src), "dsm")
        for w2 in range(2):
            dma("pool", fb_bf.w(fb_bf.ap[:, w2].rearrange("p a b -> p (a b)")),
                d_fb[:, w2 * 4096:(w2 + 1) * 4096], "dw", max_dma_last_dim=8192)
        for tile in range(2):
            for cs_ in range(2):
                bk = Bank()
                mm([(bk.v[0:128], c64[cs_ * 128:(cs_ + 1) * 128], ftwbd[tile * 128:(tile + 1) * 128], True, True)])
                cp(wcs_bf[tile, cs_], bk.v[0:128])
                bk.free()
        for kb in range(4):
            for c16 in range(8):
                bks = [Bank(), Bank()]
                for ci in range(16):
                    for h_ in range(2):
                        c = c16 * 16 + ci
                        lap = X_sb.ap[h_ * 64:(h_ + 1) * 64, :, c]
                        rap = cs1_bf.ap[h_ * 64:(h_ + 1) * 64, kb, :]
                        mm([(bks[h_].v[ci * 32:(ci + 1) * 32], X_sb.w(lap), cs1_bf.w(rap), True, True)])
                for h_ in range(2):
                    ch0 = h_ * 128 + c16 * 16
                    dst = T_blk[ch0:ch0 + 16]
                    src = bks[h_].v.w(bks[h_].v.ap.rearrange("p (a b) -> p a b", a=16))
                    if h_ == 0:
                        act(dst, src, AF.Copy)
                    else:
                        cp(dst, src)
                    bks[h_].free()
            for half in range(2):
                bks = [Bank(), Bank()]
                for k1l in range(8):
                    kk = half * 8 + k1l
                    k1 = kb * 16 + kk
                    for tile in range(2):
                        o = bks[tile].v[k1l * 64:(k1l + 1) * 64]
                        lre = T_blk.w(T_blk.ap[:, tile * 128:(tile + 1) * 128, kk])
                        lim = T_blk.w(T_blk.ap[:, tile * 128:(tile + 1) * 128, 16 + kk])
                        mm([(o, lre, fb_bf[0, k1], True, False), (o, lim, fb_bf[1, k1], False, True)])
                for tile in range(2):
                    src = bks[tile].v.w(bks[tile].v.ap.rearrange("p (k q j) -> p q j k", k=8, q=2))
                    k1s = kb * 16 + half * 8
                    dst = PQ.w(PQ.ap[:, tile].rearrange("p q (j k) -> p q j k", k=64)[:, :, :, k1s:k1s + 8])
                    if tile == 0:
                        act(dst, src, AF.Copy)
                    else:
                        cp(dst, src)
                    bks[tile].free()
        for blk in range(NBLK):
            bs = slice(blk * 512, (blk + 1) * 512)
            for tile in range(2):
                bk = Bank()
                mm([(bk.v, wcs_bf[tile, 0], PQ[tile, 0, bs], True, False),
                    (bk.v, wcs_bf[tile, 1], PQ[tile, 1, bs], False, True)])
                act(yft[tile], bk.v, AF.Identity, bias=pp[PP_FTB + tile:PP_FTB + tile + 1])
                bk.free()
            sumsq_rstd(yft, 2, dtot=256)
            for tile in range(2):
                stt(yT_bf[6 + tile, bs], yft[tile], der[DGMO + 6 + tile:DGMO + 7 + tile], Rb, ALU.mult, ALU.mult)

        chk(6)
        a_re = ar(32 * K, [2, T], F32)
        i_re = ar(48 * K, [2, T], F32)
        hsT = ar(64 * K, [4, T], F32)
        hb_t = ar(96 * K, [T], F32)
        gg_b = ar(104 * K, [4, 512], F32)
        for t in range(4):
            for d in range(2):
                dma("sp", a_re[d], aD.w(aD.ap[d, t * 128:(t + 1) * 128, :]), "dsm",
                    reads=[aD[d, t * 128:(t + 1) * 128]])
                dma("sp", i_re[d], inpD.w(inpD.ap[d, t * 128:(t + 1) * 128, :]), "dsm",
                    reads=[inpD[d, t * 128:(t + 1) * 128]])
            scan(hsT[t], a_re[0], i_re[0], hin[0, t:t + 1])
            scan(hb_t.w(hb_t.ap[:, ::-1]), a_re[1].w(a_re[1].ap[:, ::-1]), i_re[1].w(i_re[1].ap[:, ::-1]),
                 hin[1, t:t + 1])
            tt(hsT[t], hsT[t], hb_t, ALU.add)
        ytmp = ar(32 * K, [4, 512], F32)
        for blk in range(NBLK):
            bs = slice(blk * 512, (blk + 1) * 512)
            dma("sp", gg_b, ggD.w(ggD.ap[:, bs].rearrange("(t p) n -> p t n", p=128)), "dsm")
            tt(ytmp, hsT[:, bs], gg_b, ALU.mult)
            sumsq_rstd(ytmp, 4, dtot=512)
            for t in range(4):
                stt(yT_bf[2 + t, bs], ytmp[t], der[DGMO + 2 + t:DGMO + 3 + t], Rb, ALU.mult, ALU.mult)

        chk(7)
        wout_bf = ar(32 * K, [8, D], BF16)
        o_sb = ar(48 * K, [8, 512], F32)
        for kc in range(8):
            dma("pool", wout_bf[kc], d_wout[l][kc * 128:(kc + 1) * 128, :], "dw")
        for blk in range(NBLK):
            bs = slice(blk * 512, (blk + 1) * 512)
            for n in range(8):
                bk = Bank()
                mm([(bk.v, wout_bf[kc, n * 128:(n + 1) * 128], yT_bf[kc, bs], kc == 0, kc == 7) for kc in range(8)])
                if n % 2 == 0:
                    act(o_sb[n], bk.v, AF.Copy)
                else:
                    cp(o_sb[n], bk.v)
                bk.free()
            sumsq_rstd(o_sb, 8, dtot=D)
            for n in range(8):
                tt(o_sb[n], o_sb[n], Rb, ALU.mult)
                stt(xT[n, bs], o_sb[n], der[DGT1 + n:DGT1 + n + 1], xT[n, bs], ALU.mult, ALU.add)

        chk(8)
        f1sq = ar(0, [32, 1024], BF16)
        hT2 = ar(64 * K, [8, 1024], BF16)
        w1b = [ar(80 * K, [8, 512], BF16), ar(88 * K, [8, 512], BF16)]
        f_sb = ar(64 * K, [8, 1024], F32)
        w2b = [ar(96 * K, [32, 128], BF16), ar(104 * K, [32, 128], BF16)]
        tmpF = ar(112 * K, [512], F32)
        tmpG = ar(114 * K, [512], F32)
        for hf in range(2):
            for b2 in range(2):
                blk = hf * 2 + b2
                bs = slice(blk * 512, (blk + 1) * 512)
                sumsq_rstd(xT[:, bs], 8, dtot=D)
                for kc in range(8):
                    tmp = tmpF if kc % 2 == 0 else tmpG
                    tt(tmp, xT[kc, bs], Rb, ALU.mult)
                    act(hT2[kc, b2 * 512:(b2 + 1) * 512], tmp, AF.Identity,
                        scale=der[DG2 + kc:DG2 + kc + 1], bias=SH2[kc:kc + 1])
            for mg in range(8):
                wb = w1b[mg % 2]
                for kc in range(8):
                    dma("pool", wb[kc], d_wff1[l][kc * 128:(kc + 1) * 128, mg * 512:(mg + 1) * 512], "dw")
                for mc in range(4):
                    m = mg * 4 + mc
                    for b2 in range(2):
                        cs = slice(b2 * 512, (b2 + 1) * 512)
                        bk = Bank()
                        mm([(bk.v, wb[kc, mc * 128:(mc + 1) * 128], hT2[kc, cs], kc == 0, kc == 7)
                            for kc in range(8)])
                        tmp = tmpF if (mc * 2 + b2) % 2 == 0 else tmpG
                        act(tmp, bk.v, AF.Square)
                        stt(f1sq[m, cs], bk.v, 0.0, tmp, ALU.is_gt, ALU.mult)
                        bk.free()
            for n in range(8):
                wb = w2b[n % 2]
                dma("pool", wb, d_wff2[l][:, n * 128:(n + 1) * 128].rearrange("(mc p) n -> p mc n", p=128), "dw")
                for b2 in range(2):
                    cs = slice(b2 * 512, (b2 + 1) * 512)
                    bk = Bank()
                    mm([(bk.v, wb[mc], f1sq[mc, cs], mc == 0, mc == 31) for mc in range(32)])
                    if b2 == 0:
                        act(f_sb[n, cs], bk.v, AF.Copy)
                    else:
                        cp(f_sb[n, cs], bk.v)
                    bk.free()
            for b2 in range(2):
                blk = hf * 2 + b2
                bs = slice(blk * 512, (blk + 1) * 512)
                cs = slice(b2 * 512, (b2 + 1) * 512)
                sumsq_rstd(f_sb[:, cs], 8, dtot=D)
                for n in range(8):
                    tt(f_sb[n, cs], f_sb[n, cs], Rb, ALU.mult)
                    stt(xT[n, bs], f_sb[n, cs], der[DGT2 + n:DGT2 + n + 1], xT[n, bs], ALU.mult, ALU.add)

    try:
        for l in range(depth_run):
            layer(l)
    except StopBuild:
        pass

    for c in range(8):
        out_sems.append(S.op("sp", (lambda e, c=c: e.dma_start(out=d_out[c * 128:(c + 1) * 128, :],
                                                                 in_=xT[c].ap)),
                             [xT[c]], [], dma="dout"))
    final_sem, final_val = out_sems[-1]

    sems = {}
    for name in S.cnt:
        sems[name] = es.enter_context(nc.semaphore(name))
    with nc.Block() as block:
        @block.tensor
        def _(e):
            S.emit("pe", e, sems)

        @block.scalar
        def _(e):
            S.emit("act", e, sems)

        @block.vector
        def _(e):
            S.emit("dve", e, sems)

        @block.gpsimd
        def _(e):
            S.emit("pool", e, sems)

        @block.sync
        def _(e):
            S.emit("sp", e, sems)
            e.wait_ge(sems[final_sem], final_val)
    es.close()
    return nc


def _pcol(v):
    v = np.asarray(v, np.float32)
    return np.ascontiguousarray(v.reshape(-1, 128).T)


def _host_constants():
    s1 = np.arange(64)[:, None].astype(np.float64)
    k1 = np.arange(64)[None, :].astype(np.float64)
    ang = 2 * np.pi * s1 * k1 / 64.0
    cosm, sinm = np.cos(ang), np.sin(ang)
    cs1 = np.zeros((64, 4, 32), np.float32)
    for kb in range(4):
        cs1[:, kb, 0:16] = cosm[:, kb * 16:(kb + 1) * 16]
        cs1[:, kb, 16:32] = sinm[:, kb * 16:(kb + 1) * 16]
    cs1 = cs1.reshape(64, 128)
    cs1 = np.ascontiguousarray(np.concatenate([cs1, cs1], axis=0))
    m = np.arange(64)[:, None].astype(np.float64)
    jj = np.arange(64)[None, :].astype(np.float64)
    a64 = 2 * np.pi * m * jj / 64.0
    sc = 1.0 / math.sqrt(SEQ * 64.0)
    c64 = np.zeros((128, 256), np.float32)
    for g in range(2):
        c64[g * 64:(g + 1) * 64, g * 64:(g + 1) * 64] = np.cos(a64) * sc
        c64[g * 64:(g + 1) * 64, 128 + g * 64:128 + (g + 1) * 64] = -np.sin(a64) * sc
    fbs = []
    for j in range(4):
        s2 = np.arange(128)[:, None, None].astype(np.float64)
        kk1 = np.arange(64)[None, :, None].astype(np.float64)
        k2 = (32 * j + np.arange(32))[None, None, :].astype(np.float64)
        th = 2 * np.pi * s2 * (kk1 + 64.0 * k2) / SEQ
        cb, sb_ = np.cos(th), np.sin(th)
        f1 = np.concatenate([cb, sb_], axis=2)
        f2 = np.concatenate([-sb_, cb], axis=2)
        fb = np.concatenate([f1.reshape(128, 4096), f2.reshape(128, 4096)], axis=1).astype(np.float32)
        fbs.append(np.ascontiguousarray(fb))
    masks = []
    for j in range(4):
        mk = np.zeros((128, 12), np.float32)
        mk[:, j] = 1.0
        if j > 0:
            mk[:, 4 + j - 1] = 1.0
        if j < 3:
            mk[:, 8 + j + 1] = 1.0
        masks.append(mk)
    return cs1, c64, fbs, masks


def _prep_inputs(inputs, depth_run=DEPTH):
    f = lambda k: np.asarray(inputs[k], np.float32)
    x, c = f("x"), f("c")
    pp = np.zeros((DEPTH, 128, NP), np.float32)
    gmln = np.zeros((DEPTH, 128, 512), np.float32)
    wsT = np.zeros((DEPTH, 128, 512), np.float32)
    bsrow = np.zeros((DEPTH, 1, 512), np.float32)
    rgw = np.zeros((DEPTH, 128, 16, 128), np.float32)
    ftwbd = np.zeros((DEPTH, 128, 256), np.float32)
    for l in range(DEPTH):
        pp[l, :, PP_BADA:PP_BADA + 48] = _pcol(f("b_ada")[l])
        pp[l, :, PP_GPRE1:PP_GPRE1 + 8] = _pcol(f("g_pre_mix")[l])
        pp[l, :, PP_GPOST1:PP_GPOST1 + 8] = _pcol(f("g_post_mix")[l])
        pp[l, :, PP_GPRE2:PP_GPRE2 + 8] = _pcol(f("g_pre_ff")[l])
        pp[l, :, PP_GPOST2:PP_GPOST2 + 8] = _pcol(f("g_post_ff")[l])
        pp[l, :, PP_GMO:PP_GMO + 8] = _pcol(f("g_mix_out")[l])
        for k in range(4):
            pp[l, :, PP_CW + 4 * k:PP_CW + 4 * k + 4] = _pcol(f("rg_conv_w")[l, k])
        pp[l, :, PP_CB:PP_CB + 4] = _pcol(f("rg_conv_b")[l])
        for d in range(2):
            pp[l, :, PP_BA + 4 * d:PP_BA + 4 * d + 4] = _pcol(f("rg_b_a")[l, d].reshape(-1))
            pp[l, :, PP_BX + 4 * d:PP_BX + 4 * d + 4] = _pcol(f("rg_b_x")[l, d].reshape(-1))
            pp[l, :, PP_LAM + 4 * d:PP_LAM + 4 * d + 4] = _pcol(f("rg_lam")[l, d].reshape(-1))
        pp[l, :, PP_FTB:PP_FTB + 2] = _pcol(f("ft_b")[l].reshape(-1))
        gmln[l, :, 0:256] = f("gm_ln_g")[l][None, :]
        gmln[l, :, 256:512] = f("gm_ln_b")[l][None, :]
        for h in range(4):
            wsT[l, :, h * 128:(h + 1) * 128] = f("gm_w_s")[l, h].T
            bsrow[l, 0, h * 128:(h + 1) * 128] = f("gm_b_s")[l, h]
        for d in range(2):
            for gi, nm in enumerate(("rg_w_a", "rg_w_x")):
                for t in range(4):
                    for hh in range(2):
                        rgw[l, hh * 64:(hh + 1) * 64, (d * 2 + gi) * 4 + t, hh * 64:(hh + 1) * 64] = \
                            f(nm)[l, d, 2 * t + hh]
        for tile in range(2):
            for g in range(2):
                ftwbd[l, g * 64:(g + 1) * 64, tile * 128 + g * 64:tile * 128 + (g + 1) * 64] = \
                    f("ft_w")[l, 2 * tile + g]
    cs1, c64, fbs, masks = _host_constants()
    shared = {
        "w_ada": f("w_ada")[:depth_run], "w_in": f("w_in")[:depth_run if STOP >= 1 else 1],
        "w_out": f("w_out")[:depth_run if STOP >= 7 else 1], "w_ff1": f("w_ff1")[:depth_run if STOP >= 8 else 1],
        "w_ff2": f("w_ff2")[:depth_run if STOP >= 8 else 1],
        "pp": pp, "gmln": gmln, "wsT": wsT, "bsrow": bsrow, "rgw": rgw.reshape(DEPTH, 128, 2048),
        "ftwbd": ftwbd, "cs1": cs1, "c64bd": c64,
    }
    in_maps = []
    for k in range(8):
        b, j = k // 4, k % 4
        m = dict(shared)
        m["xT"] = np.ascontiguousarray(x[b, j * T:(j + 1) * T, :].T)
        m["cT"] = _pcol(c[b])
        m["fb"] = fbs[j]
        m["masks"] = masks[j]
        in_maps.append(m)
    return in_maps


_NC_CACHE = {}


def kernel(**inputs):
    depth_run = int(inputs.pop("_depth_run", DEPTH)) if "_depth_run" in inputs else DEPTH
    in_maps = _prep_inputs(inputs, depth_run)
    if (depth_run, STOP) not in _NC_CACHE:
        _NC_CACHE[(depth_run, STOP)] = build_program(depth_run)
    nc = _NC_CACHE[(depth_run, STOP)]
    res = run_bass_kernel_spmd(nc, in_maps, core_ids=list(range(8)))
    out = np.zeros((2, SEQ, D), np.float32)
    for k in range(8):
        b, j = k // 4, k % 4
        out[b, j * T:(j + 1) * T, :] = np.asarray(res.results[k]["outT"]).T
    return out
```
